# Optimizing a Trainium2 kernel written in Bass

```python
import jax, jax.numpy as jnp
from jax import lax
import numpy as np

D_MODEL = 1024
BATCH = 8
SEQ = 8192
DEPTH = 4

POOL_WINDOWS = (2, 4, 8, 16)
POOL_GROUPS = len(POOL_WINDOWS)
POOL_GROUP_DIM = D_MODEL // 8
POOL_DIM = POOL_GROUPS * POOL_GROUP_DIM
POOL_WMAX = max(POOL_WINDOWS)
HEAD_DIM = 64
N_Q_HEADS = D_MODEL // 128
N_KV_HEADS = 2
GQA_GROUP = N_Q_HEADS // N_KV_HEADS
ATTN_DIM = N_Q_HEADS * HEAD_DIM
KV_DIM = N_KV_HEADS * HEAD_DIM
WINDOW = 128
BLOCK = 128
ROPE_THETA = 500000.0
ROT_DIM = HEAD_DIM // 4
N_BRANCHES = 2
IN_DIM = POOL_DIM + ATTN_DIM + 2 * KV_DIM + N_BRANCHES * D_MODEL
D_FF = 2816
EPS = 1e-6

kernel_name = "hybrid_pool_swa_macaron"


def rmsnorm(x, g):
    xf = x.astype(jnp.float32)
    y = xf * lax.rsqrt(jnp.mean(xf * xf, axis=-1, keepdims=True) + EPS)
    return (y * g.astype(jnp.float32)).astype(x.dtype)


def swiglu(h, w_gu, w_down):
    g, u = jnp.split(h @ w_gu, 2, axis=-1)
    return (jax.nn.silu(g) * u) @ w_down


def pool_mixer(u, w_grp, scale):
    B, S, _ = u.shape
    uf = u.astype(jnp.float32)
    c = jnp.cumsum(uf, axis=1)
    c_pad = jnp.pad(c, ((0, 0), (POOL_WMAX, 0), (0, 0)))
    t = jnp.arange(S)
    outs = []
    for g, w in enumerate(POOL_WINDOWS):
        lo, hi = g * POOL_GROUP_DIM, (g + 1) * POOL_GROUP_DIM
        win_sum = c[:, :, lo:hi] - c_pad[:, POOL_WMAX - w:POOL_WMAX - w + S, lo:hi]
        count = jnp.minimum(t + 1, w).astype(jnp.float32)[None, :, None]
        outs.append(win_sum / count - uf[:, :, lo:hi])
    d = jnp.stack(outs, axis=2).astype(u.dtype)
    y = jnp.einsum('bsgc,gcd->bsgd', d, w_grp).reshape(B, S, POOL_DIM)
    return y * scale


def partial_rope(x, cos, sin):
    half = ROT_DIM // 2
    x1 = x[..., :half].astype(jnp.float32)
    x2 = x[..., half:ROT_DIM].astype(jnp.float32)
    c = cos[None, :, None, :]
    s = sin[None, :, None, :]
    rot = jnp.concatenate([x1 * c - x2 * s, x2 * c + x1 * s], axis=-1).astype(x.dtype)
    return jnp.concatenate([rot, x[..., ROT_DIM:]], axis=-1)


def swa_sink_attention(q, k, v, sinks):
    B, S = q.shape[0], q.shape[1]
    nb = S // BLOCK
    qb = q.reshape(B, nb, BLOCK, N_KV_HEADS, GQA_GROUP, HEAD_DIM)

    def with_prev(t):
        tb = t.reshape(B, nb, BLOCK, N_KV_HEADS, HEAD_DIM)
        prev = jnp.pad(tb[:, :-1], ((0, 0), (1, 0), (0, 0), (0, 0), (0, 0)))
        return jnp.concatenate([prev, tb], axis=2)

    kb, vb = with_prev(k), with_prev(v)
    s = jnp.einsum('bnqhgd,bnkhd->bnhgqk', qb, kb).astype(jnp.float32) * (HEAD_DIM ** -0.5)
    qi = jnp.arange(BLOCK)[:, None]
    ki = jnp.arange(2 * BLOCK)[None, :]
    diff = qi + BLOCK - ki
    band = (diff >= 0) & (diff < WINDOW)
    valid = (jnp.arange(nb)[:, None, None] > 0) | (ki[None] >= BLOCK)
    mask = band[None] & valid
    s = jnp.where(mask[None, :, None, None], s, -jnp.inf)
    sink = sinks.astype(jnp.float32).reshape(1, 1, N_KV_HEADS, GQA_GROUP, 1, 1)
    m = jnp.maximum(jnp.max(s, axis=-1, keepdims=True), sink)
    p = jnp.exp(s - m)
    denom = jnp.sum(p, axis=-1, keepdims=True) + jnp.exp(sink - m)
    p = (p / denom).astype(v.dtype)
    o = jnp.einsum('bnhgqk,bnkhd->bnqhgd', p, vb)
    return o.reshape(B, S, ATTN_DIM)


def setup_inputs(seed: int = 0) -> dict:
    key = jax.random.key(seed)
    ks = jax.random.split(key, 17)
    f32 = jnp.float32

    def w(k, shape, fan_in):
        return jax.random.normal(k, shape, f32) * (fan_in ** -0.5)

    def gain(k, shape):
        return 1.0 + 0.05 * jax.random.normal(k, shape, f32)

    L = DEPTH
    return {
        "x": jax.random.normal(ks[0], (BATCH, SEQ, D_MODEL), f32),
        "ln_ffn1": gain(ks[1], (L, D_MODEL)),
        "w_ffn1_gu": w(ks[2], (L, D_MODEL, 2 * D_FF), D_MODEL),
        "w_ffn1_down": w(ks[3], (L, D_FF, D_MODEL), D_FF),
        "ln_mix": gain(ks[4], (L, D_MODEL)),
        "w_in": w(ks[5], (L, D_MODEL, IN_DIM), D_MODEL),
        "pool_w": w(ks[6], (L, POOL_GROUPS, POOL_GROUP_DIM, POOL_GROUP_DIM), POOL_GROUP_DIM),
        "pool_scale": 1.0 + 0.1 * jax.random.normal(ks[7], (L, POOL_DIM), f32),
        "w_pool_branch": w(ks[8], (L, POOL_DIM, D_MODEL), POOL_DIM),
        "q_norm": gain(ks[9], (L, HEAD_DIM)),
        "k_norm": gain(ks[10], (L, HEAD_DIM)),
        "sinks": 0.5 * jax.random.normal(ks[11], (L, N_Q_HEADS), f32),
        "w_attn_branch": w(ks[12], (L, ATTN_DIM, D_MODEL), ATTN_DIM),
        "w_out": w(ks[13], (L, D_MODEL, D_MODEL), D_MODEL),
        "ln_ffn2": gain(ks[14], (L, D_MODEL)),
        "w_ffn2_gu": w(ks[15], (L, D_MODEL, 2 * D_FF), D_MODEL),
        "w_ffn2_down": w(ks[16], (L, D_FF, D_MODEL), D_FF),
    }


def reference(x, ln_ffn1, w_ffn1_gu, w_ffn1_down, ln_mix, w_in, pool_w, pool_scale,
              w_pool_branch, q_norm, k_norm, sinks, w_attn_branch, w_out,
              ln_ffn2, w_ffn2_gu, w_ffn2_down):
    B, S, _ = x.shape
    pos = jnp.arange(S, dtype=jnp.float32)
    inv_freq = ROPE_THETA ** (-jnp.arange(0, ROT_DIM, 2, dtype=jnp.float32) / ROT_DIM)
    ang = pos[:, None] * inv_freq[None, :]
    cos, sin = jnp.cos(ang), jnp.sin(ang)
    splits = [POOL_DIM, POOL_DIM + ATTN_DIM, POOL_DIM + ATTN_DIM + KV_DIM,
              POOL_DIM + ATTN_DIM + 2 * KV_DIM]

    for l in range(DEPTH):
        x = x + 0.5 * swiglu(rmsnorm(x, ln_ffn1[l]), w_ffn1_gu[l], w_ffn1_down[l])

        h = rmsnorm(x, ln_mix[l])
        z = h @ w_in[l]
        u_pool, q, k, v, gate_logits = jnp.split(z, splits, axis=-1)

        a = pool_mixer(u_pool, pool_w[l], pool_scale[l]) @ w_pool_branch[l]

        q = rmsnorm(q.reshape(B, S, N_Q_HEADS, HEAD_DIM), q_norm[l])
        k = rmsnorm(k.reshape(B, S, N_KV_HEADS, HEAD_DIM), k_norm[l])
        v = v.reshape(B, S, N_KV_HEADS, HEAD_DIM)
        q = partial_rope(q, cos, sin)
        k = partial_rope(k, cos, sin)
        b = swa_sink_attention(q, k, v, sinks[l]) @ w_attn_branch[l]

        g_pool, g_attn = jnp.split(jax.nn.sigmoid(gate_logits), N_BRANCHES, axis=-1)
        x = x + (g_pool * a + g_attn * b) @ w_out[l]

        x = x + 0.5 * swiglu(rmsnorm(x, ln_ffn2[l]), w_ffn2_gu[l], w_ffn2_down[l])
    return x
```

```python
import os
import numpy as np
import concourse.bass as bass
import concourse.mybir as mybir
from concourse.bass_utils import run_bass_kernel_spmd

F32 = mybir.dt.float32
BF16 = mybir.dt.bfloat16
AF = mybir.ActivationFunctionType
ALU = mybir.AluOpType

D = 1024
DFF = 2816
NFC = 22
T = 256
EPS = 1e-6
WIN_COLS = 3456
C_U, C_Q, C_K, C_V, C_GP, C_GA = 0, 512, 1024, 1280, 1408, 2432
NVL = 38
POOL_W = (2, 4, 8, 16)
SAME_SYNC = True
MIXSTOP = int(os.environ.get('MIXSTOP', '9'))
MIXSKIP = int(os.environ.get('MIXSKIP', '0'))


class Buf:
    __slots__ = ("w", "r")

    def __init__(self):
        self.w = None
        self.r = {}


class Eng:
    def __init__(self, nc, e, name, is_pe=False):
        self.e = e
        self.sem = nc.alloc_semaphore("p_" + name)
        self.cnt = 0
        self.waited = {}
        self.is_pe = is_pe

    def wait(self, ev):
        if ev is None:
            return
        sem, val = ev
        if sem is self.sem and (self.is_pe or not SAME_SYNC):
            return
        k = sem.num
        if self.waited.get(k, 0) >= val:
            return
        self.e.wait_ge(sem, val)
        self.waited[k] = val


class DSem:
    def __init__(self, nc, name):
        self.sem = nc.alloc_semaphore(name)
        self.cnt = 0


def _flat(bs):
    out = []
    for b in bs:
        if isinstance(b, (list, tuple)):
            out.extend(_flat(b))
        else:
            out.append(b)
    return out


def _pre(E, reads, writes):
    for b in reads:
        E.wait(b.w)
    for b in writes:
        E.wait(b.w)
        for ev in b.r.values():
            E.wait(ev)


def _post(ev, reads, writes):
    k = ev[0].num
    for b in reads:
        b.r[k] = ev
    for b in writes:
        b.w = ev
        b.r = {}


def op(E, fn, reads=(), writes=()):
    reads = _flat(reads)
    writes = _flat(writes)
    _pre(E, reads, writes)
    ins = fn(E.e)
    E.cnt += 1
    ins.then_inc(E.sem, 1)
    ev = (E.sem, E.cnt)
    _post(ev, reads, writes)
    return ev


def mmg(PE, out_ap, pairs, reads=(), writes=()):
    reads = _flat(reads)
    writes = _flat(writes)
    _pre(PE, reads, writes)
    n = len(pairs)
    ins = None
    for i, (l, r) in enumerate(pairs):
        ins = PE.e.matmul(out_ap, lhsT=l, rhs=r, start=(i == 0), stop=(i == n - 1))
    PE.cnt += 1
    ins.then_inc(PE.sem, 1)
    ev = (PE.sem, PE.cnt)
    _post(ev, reads, writes)
    return ev


def dma(Q, ds, pairs, reads=(), writes=()):
    reads = _flat(reads)
    writes = _flat(writes)
    _pre(Q, reads, writes)
    for o, i in pairs:
        Q.e.dma_start(out=o, in_=i).then_inc(ds.sem, 16)
        ds.cnt += 16
    ev = (ds.sem, ds.cnt)
    _post(ev, reads, writes)
    return ev


class Carver:
    def __init__(self, t, nbytes):
        self.t = t
        self.n = nbytes
        self.off = 0

    def take(self, shape, dt):
        esz = 4 if dt == F32 else 2
        n = int(np.prod(shape)) * esz
        n_al = (n + 63) // 64 * 64
        assert self.off + n_al <= self.n, ("carve overflow", self.off, n_al, self.n)
        a = self.t[:, self.off // 2:(self.off + n) // 2]
        self.off += n_al
        if dt == F32:
            a = a.bitcast(F32)
        if len(shape) == 2:
            a = a.rearrange("p (a b) -> p a b", a=shape[0])
        elif len(shape) == 3:
            a = a.rearrange("p (a b c) -> p a b c", a=shape[0], b=shape[1])
        return a


def build(S, L, phases=None):
    NT = S // T
    nc = bass.Bass("TRN2", target_bir_lowering=False)
    dt_ = nc.dram_tensor
    xT = dt_("xT", [D, S], F32, kind="ExternalInput").ap()
    w1gu = dt_("w1gu", [L, D, 2 * DFF], F32, kind="ExternalInput").ap()
    w1d = dt_("w1d", [L, DFF, D], F32, kind="ExternalInput").ap()
    w2gu = dt_("w2gu", [L, D, 2 * DFF], F32, kind="ExternalInput").ap()
    w2d = dt_("w2d", [L, DFF, D], F32, kind="ExternalInput").ap()
    win = dt_("win", [L, D, WIN_COLS], F32, kind="ExternalInput").ap()
    wpool = dt_("wpool", [L, 4, 128, 128], F32, kind="ExternalInput").ap()
    wpb = dt_("wpb", [L, 512, D], F32, kind="ExternalInput").ap()
    wab = dt_("wab", [L, 512, D], F32, kind="ExternalInput").ap()
    wout = dt_("wout", [L, D, D], F32, kind="ExternalInput").ap()
    cvec_d = dt_("cvec", [128, NVL * L], F32, kind="ExternalInput").ap()
    rope_d = dt_("rope", [128, 2, S], F32, kind="ExternalInput").ap()
    consts_d = dt_("consts", [128, 768], F32, kind="ExternalInput").ap()
    invc_d = dt_("invc", [128, 64], F32, kind="ExternalInput").ap()
    outT = dt_("outT", [D, S], F32, kind="ExternalOutput").ap()
    xs = dt_("xs", [D, S], F32, kind="Internal").ap()

    def xview(a):
        return a.rearrange("(k p) s -> p k s", p=128)

    RA_B, RB_B, RW_B = 90112, 45056, 48128
    RA = nc.alloc_sbuf_tensor("RA", [128, RA_B // 2], BF16)
    RB = nc.alloc_sbuf_tensor("RB", [128, RB_B // 2], BF16)
    RW = nc.alloc_sbuf_tensor("RW", [128, RW_B // 2], BF16)
    consts = nc.alloc_sbuf_tensor("consts_sb", [128, 768], BF16)
    cvec = nc.alloc_sbuf_tensor("cvecs", [128, NVL * L], F32)
    invc = nc.alloc_sbuf_tensor("invcs", [128, 4, 16], F32)
    esink = nc.alloc_sbuf_tensor("esink", [128, 8 * L], F32)
    onesf = nc.alloc_sbuf_tensor("onesf", [128, T], F32)
    PS = nc.alloc_psum_tensor("PS", [128, 8, 512], F32)

    ones_k = consts[:, 0:128]
    blk64 = consts[:, 128:256]
    Pm = consts[:, 256:384]
    ident = consts[:, 384:512]
    mask = consts[:, 512:768]

    Wgu = RA[:, :].rearrange("p (k f) -> p k f", k=8)
    Wd = RB[:, :].rearrange("p (j d) -> p j d", j=NFC)
    ca = Carver(RA, RA_B)
    Win = ca.take([8, WIN_COLS], BF16)
    Wout = ca.take([8, D], BF16)
    Wpb = ca.take([4, D], BF16)
    Wab = ca.take([4, D], BF16)
    Wpool = ca.take([4, 128], BF16)

    cw = Carver(RW, RW_B)
    xb = [cw.take([8, T], F32) for _ in range(2)]
    sq = cw.take([8, T], BF16)
    hb = [cw.take([8, T], BF16) for _ in range(2)]
    rs = cw.take([1, T], F32)[:, 0, :]
    rstd = cw.take([1, T], F32)[:, 0, :]
    common_off = cw.off
    act = cw.take([NFC, T], BF16)
    sg = [cw.take([1, T], F32)[:, 0, :] for _ in range(2)]
    cb = Carver(RB, RB_B)
    cw2 = Carver(RW, RW_B)
    cw2.off = common_off
    sqq = cb.take([6, T], BF16)
    qg = cb.take([6, T], F32)
    rq = cb.take([6, T], F32)
    qn = cb.take([6, T], BF16)
    qT = cb.take([4, T], BF16)
    kT = cb.take([2, 128 + T], BF16)
    vb = cb.take([3, 2, 66], BF16)
    ub = cb.take([4, 16 + T], F32)
    sA = cb.take([1, 16 + T], F32)[:, 0, :]
    sB = cb.take([1, 16 + T], F32)[:, 0, :]
    sC = cb.take([1, 16 + T], F32)[:, 0, :]
    dp = cb.take([4, T], BF16)
    ys = cb.take([4, T], BF16)
    ebuf = [cb.take([1, 256], BF16)[:, 0, :] for _ in range(3)]
    pT = [cb.take([1, 256], BF16)[:, 0, :] for _ in range(3)]
    den = cb.take([1, 8], F32)[:, 0, :]
    rec = cb.take([1, 8], F32)[:, 0, :]
    fix = cb.take([1, 16], F32)[:, 0, :]
    on = [cb.take([8, 64], BF16) for _ in range(2)]
    oT = cb.take([4, T], BF16)
    tg = [cw2.take([1, 512], F32)[:, 0, :] for _ in range(2)]
    m1 = [cw2.take([1, T], F32)[:, 0, :] for _ in range(2)]
    m2 = [cw2.take([1, T], F32)[:, 0, :] for _ in range(2)]
    mb = cw2.take([8, T], BF16)
    cs = [cw2.take([2, T], F32) for _ in range(2)]

    PE = Eng(nc, nc.tensor, "pe", is_pe=True)
    ACT = Eng(nc, nc.scalar, "act")
    DVE = Eng(nc, nc.vector, "dve")
    POOL = Eng(nc, nc.gpsimd, "pool")
    SP = Eng(nc, nc.sync, "sp")
    ds_c = DSem(nc, "d_c")
    ds_c2 = DSem(nc, "d_c2")
    ds_wa = DSem(nc, "d_wa")
    ds_wb = DSem(nc, "d_wb")
    ds_xl = [DSem(nc, "d_xl%d" % i) for i in range(2)]
    ds_xs = [DSem(nc, "d_xs%d" % i) for i in range(2)]
    ds_cs = [DSem(nc, "d_cs%d" % i) for i in range(2)]

    B_c1, B_c2 = Buf(), Buf()
    B_const = [B_c1, B_c2]
    B_RA, B_RB = Buf(), Buf()
    B_xb = [[Buf() for _ in range(8)] for _ in range(2)]
    B_sq = Buf()
    B_h = [[Buf() for _ in range(8)] for _ in range(2)]
    B_rs, B_rstd = Buf(), Buf()
    B_act = [Buf() for _ in range(NFC)]
    B_sg = [Buf(), Buf()]
    B_ps = [Buf() for _ in range(8)]
    B_psh = [[Buf(), Buf()] for _ in range(8)]
    B_xs = [Buf() for _ in range(NT)]
    B_in = [Buf() for _ in range(NT)]
    B_out = [Buf() for _ in range(NT)]
    B_sqq, B_qg, B_rq, B_qn = Buf(), [Buf() for _ in range(6)], [Buf() for _ in range(6)], Buf()
    B_qT, B_kT, B_kTh, B_vb, B_vbh = Buf(), Buf(), Buf(), [Buf(), Buf()], Buf()
    B_ub, B_ubh, B_sA, B_sB, B_sC, B_dp, B_ys = Buf(), Buf(), Buf(), Buf(), Buf(), [Buf() for _ in range(4)], [Buf() for _ in range(4)]
    B_e, B_pT = [Buf() for _ in range(3)], [Buf() for _ in range(3)]
    B_den, B_rec, B_fix = Buf(), Buf(), Buf()
    B_on, B_oT = [Buf(), Buf()], [Buf(), Buf()]
    B_cs = [Buf(), Buf()]
    B_tg, B_m1, B_m2 = [Buf(), Buf()], [Buf(), Buf()], [Buf(), Buf()]
    B_m = [Buf() for _ in range(8)]
    B_esink = Buf()

    dma(POOL, ds_c, [(consts[:, :], consts_d)], writes=[B_c1])
    dma(SP, ds_c2, [(cvec[:, :], cvec_d), (invc[:, :, :], invc_d.rearrange("p (g t) -> p g t", g=4))], writes=[B_c2])
    B_c3 = Buf()
    B_const.append(B_c3)
    op(POOL, lambda e: e.memset(onesf[:, :], 1.0), writes=[B_c3])
    for l in range(L):
        op(ACT, lambda e, l=l: e.activation(out=esink[:, 8 * l:8 * l + 8],
                                            in_=cvec[:, NVL * l + 30:NVL * l + 38], func=AF.Exp),
           reads=[B_const], writes=[B_esink])

    def load(i, src_v, src_bufs):
        par = i % 2
        dma(SP, ds_xl[par], [(xb[par], src_v[:, :, i * T:(i + 1) * T])],
            reads=[src_bufs[i]], writes=[B_xb[par]])

    def store(i, dst_v, dst_bufs):
        par = i % 2
        dma(POOL, ds_xs[par], [(dst_v[:, :, i * T:(i + 1) * T], xb[par])],
            reads=[B_xb[par]], writes=[dst_bufs[i]])

    def norm_sq(i):
        par = i % 2
        op(POOL, lambda e: e.tensor_tensor(out=sq, in0=xb[par], in1=xb[par], op=ALU.mult),
           reads=[B_xb[par]], writes=[B_sq])

    def norm_ss(i):
        mmg(PE, PS[:, 0, 0:T], [(ones_k, sq[:, k, :]) for k in range(8)],
            reads=[B_sq, B_const], writes=[B_ps[0]])

    def norm_sqrt(i):
        op(ACT, lambda e: e.activation(out=rs, in_=PS[:, 0, 0:T], func=AF.Sqrt, bias=EPS, scale=1.0),
           reads=[B_ps[0]], writes=[B_rs])

    def norm_h(i, gcol):
        par = i % 2
        op(DVE, lambda e: e.reciprocal(out=rstd, in_=rs), reads=[B_rs], writes=[B_rstd])
        for k in range(8):
            op(DVE, lambda e, k=k: e.scalar_tensor_tensor(
                out=hb[par][:, k, :], in0=xb[par][:, k, :], scalar=cvec[:, gcol + k:gcol + k + 1],
                in1=rstd, op0=ALU.mult, op1=ALU.mult),
               reads=[B_xb[par][k], B_rstd, B_const], writes=[B_h[par][k]])

    def resid(par, c, bank, bbuf):
        op(DVE, lambda e: e.scalar_tensor_tensor(
            out=xb[par][:, c, :], in0=PS[:, bank, 0:T], scalar=0.5, in1=xb[par][:, c, :],
            op0=ALU.mult, op1=ALU.add),
           reads=[bbuf, B_xb[par][c]], writes=[B_xb[par][c]])

    def ffn_phase(src_v, src_bufs, dst_v, dst_bufs, wgu_l, wd_l, gcol):
        wgu_v = wgu_l.rearrange("(k p) f -> p k f", p=128)
        wd_v = wd_l.rearrange("(j p) d -> p j d", p=128)
        dma(POOL, ds_wa, [(Wgu[:, k, :], wgu_v[:, k, :]) for k in range(8)], writes=[B_RA])
        dma(POOL, ds_wb, [(Wd[:, 0:11, :], wd_v[:, 0:11, :]), (Wd[:, 11:22, :], wd_v[:, 11:22, :])],
            writes=[B_RB])
        load(0, src_v, src_bufs)
        if NT > 1:
            load(1, src_v, src_bufs)
        norm_sq(0)
        norm_ss(0)
        norm_sqrt(0)
        norm_h(0, gcol)
        if NT > 1:
            norm_sq(1)
        for i in range(NT):
            par = i % 2
            for j in range(NFC):
                bg = 1 + 2 * (j % 2)
                bu = bg + 1
                mmg(PE, PS[:, bg, 0:T],
                    [(Wgu[:, k, j * 128:(j + 1) * 128], hb[par][:, k, :]) for k in range(8)],
                    reads=[B_RA, B_h[par]], writes=[B_ps[bg]])
                mmg(PE, PS[:, bu, 0:T],
                    [(Wgu[:, k, DFF + j * 128:DFF + (j + 1) * 128], hb[par][:, k, :]) for k in range(8)],
                    reads=[B_RA, B_h[par]], writes=[B_ps[bu]])
                s = j % 2
                op(ACT, lambda e, bg=bg, s=s: e.activation(out=sg[s], in_=PS[:, bg, 0:T], func=AF.Silu),
                   reads=[B_ps[bg]], writes=[B_sg[s]])
                op(DVE, lambda e, bu=bu, s=s, j=j: e.tensor_tensor(
                    out=act[:, j, :], in0=sg[s], in1=PS[:, bu, 0:T], op=ALU.mult),
                   reads=[B_sg[s], B_ps[bu]], writes=[B_act[j]])
            if i + 1 < NT:
                norm_ss(i + 1)
                norm_sqrt(i + 1)
            for c in range(8):
                bank = 5 + c % 3
                mmg(PE, PS[:, bank, 0:T],
                    [(Wd[:, j, c * 128:(c + 1) * 128], act[:, j, :]) for j in range(NFC)],
                    reads=[B_RB, B_act], writes=[B_ps[bank]])
                resid(par, c, bank, B_ps[bank])
                if c == 1 and i + 1 < NT:
                    norm_h(i + 1, gcol)
            store(i, dst_v, dst_bufs)
            if i + 2 < NT:
                load(i + 2, src_v, src_bufs)
                norm_sq(i + 2)

    def mix_phase(src_v, src_bufs, dst_v, dst_bufs, l):
        cv = NVL * l
        gcol = cv + 8
        win_v = win[l].rearrange("(k p) f -> p k f", p=128)
        wl = [(Win[:, k, :], win_v[:, k, :]) for k in range(8)]
        if not (MIXSKIP & 8):
            wl += [(Wout, wout[l].rearrange("(k p) f -> p k f", p=128)),
                   (Wpb, wpb[l].rearrange("(k p) f -> p k f", p=128)),
                   (Wab, wab[l].rearrange("(k p) f -> p k f", p=128))]
        if not (MIXSKIP & 4):
            wl += [(Wpool, wpool[l].rearrange("g c d -> c g d"))]
        dma(POOL, ds_wa, wl, writes=[B_RA])
        all_rb = [B_sqq, B_qg, B_rq, B_qn, B_qT, B_kT, B_kTh, B_vb, B_vbh, B_ub, B_ubh, B_sA, B_sB, B_sC,
                  B_dp, B_ys, B_e, B_pT, B_den, B_rec, B_fix, B_on, B_oT, B_cs]
        if not (MIXSKIP & 1):
            op(POOL, lambda e: e.memset(vb[:, :, :, 64:66], 1.0), reads=[], writes=[B_RB] + _flat(all_rb))
            op(POOL, lambda e: e.memset(ub[:, :, 0:16], 0.0), writes=[B_ubh])

        def load_cs(i):
            s = i % 2
            if MIXSKIP & 2:
                return
            dma(SP, ds_cs[s], [(cs[s], rope_d[:, :, i * T:(i + 1) * T])], writes=[B_cs[s]])

        load(0, src_v, src_bufs)
        load_cs(0)
        if NT > 1:
            load(1, src_v, src_bufs)
            load_cs(1)
        norm_sq(0)
        norm_ss(0)
        norm_sqrt(0)
        norm_h(0, gcol)
        if NT > 1:
            norm_sq(1)
        ring = [0]

        def nb():
            b = 4 + ring[0] % 4
            ring[0] += 1
            return b

        for i in range(NT):
            par = i % 2
            h = hb[par]
            Bh = B_h[par]
            csb = cs[i % 2]
            Bcs = B_cs[i % 2]
            for c in range(6):
                col = (C_Q + 128 * c) if c < 4 else (C_K + 128 * (c - 4))
                b = nb()
                mmg(PE, PS[:, b, 0:T], [(Win[:, k, col:col + 128], h[:, k, :]) for k in range(8)],
                    reads=[B_RA, Bh], writes=[B_ps[b]])
                if not (MIXSKIP & 16):
                    op(ACT, lambda e, b=b, c=c: e.activation(out=sqq[:, c, :], in_=PS[:, b, 0:T], func=AF.Square),
                       reads=[B_ps[b]], writes=[B_sqq])
                gc = cv + 28 + (0 if c < 4 else 1)
                if not (MIXSKIP & 32):
                    op(DVE, lambda e, b=b, c=c, gc=gc: e.scalar_tensor_tensor(
                        out=qg[:, c, :], in0=PS[:, b, 0:T], scalar=cvec[:, gc:gc + 1], in1=onesf[:, 0:T],
                        op0=ALU.mult, op1=ALU.mult),
                       reads=[B_ps[b], B_const, B_sqq], writes=[B_qg[c]])
            if MIXSTOP >= 2:
                if i > 0:
                    op(POOL, lambda e: e.tensor_copy(out=ub[:, :, 0:16], in_=ub[:, :, T:T + 16]),
                       reads=[B_ub], writes=[B_ubh])
                for g in range(4):
                    b = nb()
                    mmg(PE, PS[:, b, 0:T], [(Win[:, k, C_U + 128 * g:C_U + 128 * (g + 1)], h[:, k, :]) for k in range(8)],
                        reads=[B_RA, Bh], writes=[B_ps[b]])
                    op(ACT, lambda e, b=b, g=g: e.activation(out=ub[:, g, 16:16 + T], in_=PS[:, b, 0:T], func=AF.Copy),
                       reads=[B_ps[b], B_ubh], writes=[B_ub])
                if i > 0:
                    op(POOL, lambda e: e.tensor_copy(out=vb[:, 0, :, 0:64], in_=vb[:, 2, :, 0:64]),
                       reads=[B_vb[1]], writes=[B_vbh])
                for blk in range(2):
                    b = nb()
                    mmg(PE, PS[:, b, 0:128],
                        [(h[:, k, blk * 128:(blk + 1) * 128], Win[:, k, C_V:C_V + 128]) for k in range(8)],
                        reads=[B_RA, Bh], writes=[B_ps[b]])
                    op(DVE, lambda e, b=b, blk=blk: e.tensor_copy(
                        out=vb[:, 1 + blk, :, 0:64], in_=PS[:, b, 0:128].rearrange("p (a d) -> p a d", a=2)),
                       reads=[B_ps[b], B_vbh], writes=[B_vb[blk]])
            if MIXSTOP >= 3:
                for c in range(6):
                    bk = 1 + c // 2
                    mmg(PE, PS[:, bk, (c % 2) * T:(c % 2 + 1) * T], [(blk64, sqq[:, c, :])],
                        reads=[B_sqq, B_const], writes=[B_ps[bk]])
                if i + 1 < NT:
                    norm_ss(i + 1)
                op(ACT, lambda e: e.activation(out=rq.rearrange("p (a two) t -> p a (two t)", two=2),
                                               in_=PS[:, 1:4, :], func=AF.Sqrt, bias=EPS, scale=1.0),
                   reads=[B_ps[1], B_ps[2], B_ps[3]], writes=[B_rq])
                if i + 1 < NT:
                    norm_sqrt(i + 1)
                op(DVE, lambda e: e.reciprocal(out=rq, in_=rq), reads=[B_rq], writes=[B_rq])
                op(DVE, lambda e: e.tensor_tensor(out=qn, in0=qg, in1=rq, op=ALU.mult),
                   reads=[B_qg, B_rq], writes=[B_qn])
            if MIXSTOP >= 4:
                if i > 0:
                    op(POOL, lambda e: e.tensor_copy(out=kT[:, :, 0:128], in_=kT[:, :, T:T + 128]),
                       reads=[B_kT], writes=[B_kTh])
                for c in range(6):
                    bk = 1 + c // 2
                    pv = PS[:, bk, (c % 2) * T:(c % 2 + 1) * T]
                    mmg(PE, pv, [(Pm, qn[:, c, :])], reads=[B_qn, B_const], writes=[B_ps[bk]])
                    op(POOL, lambda e, c=c: e.tensor_tensor(out=qg[:, c, :], in0=qn[:, c, :], in1=csb[:, 0, :], op=ALU.mult),
                       reads=[B_qn, Bcs], writes=[B_qg[c]])
                    op(DVE, lambda e, c=c, pv=pv: e.tensor_tensor(out=rq[:, c, :], in0=pv, in1=csb[:, 1, :], op=ALU.mult),
                       reads=[B_ps[bk], Bcs], writes=[B_rq[c]])
                    if c < 4:
                        op(POOL, lambda e, c=c: e.tensor_tensor(out=qT[:, c, :], in0=qg[:, c, :], in1=rq[:, c, :], op=ALU.add),
                           reads=[B_qg[c], B_rq[c]], writes=[B_qT])
                    else:
                        op(POOL, lambda e, c=c: e.tensor_tensor(out=kT[:, c - 4, 128:128 + T], in0=qg[:, c, :], in1=rq[:, c, :], op=ALU.add),
                           reads=[B_qg[c], B_rq[c], B_kTh], writes=[B_kT])
            if MIXSTOP >= 5:
                chains = []
                for g in range(4):
                    u_g = ub[:, g, :]
                    steps = [(sA, B_sA, 1), (sB, B_sB, 2), (sC, B_sC, 4), (sA, B_sA, 8)][:g + 1]
                    prev, Bprev = u_g, B_ub
                    lo = 0
                    for (dst, Bd, sh) in steps:
                        lo2 = lo + sh
                        op(POOL, lambda e, dst=dst, prev=prev, lo2=lo2, sh=sh: e.tensor_tensor(
                            out=dst[:, lo2:16 + T], in0=prev[:, lo2:16 + T], in1=prev[:, lo2 - sh:16 + T - sh], op=ALU.add),
                           reads=[Bprev, B_ubh], writes=[Bd])
                        prev, Bprev, lo = dst, Bd, lo2
                    w = POOL_W[g]
                    op(DVE, lambda e, g=g, prev=prev, w=w: e.scalar_tensor_tensor(
                        out=dp[:, g, :], in0=prev[:, 16:16 + T], scalar=1.0 / w, in1=ub[:, g, 16:16 + T],
                        op0=ALU.mult, op1=ALU.subtract),
                       reads=[Bprev, B_ub], writes=[B_dp[g]])
                    if i == 0:
                        op(POOL, lambda e, g=g, prev=prev: e.tensor_tensor(
                            out=fix, in0=prev[:, 16:32], in1=invc[:, g, :], op=ALU.mult),
                           reads=[Bprev, B_const], writes=[B_fix])
                        op(POOL, lambda e, g=g: e.tensor_tensor(
                            out=dp[:, g, 0:16], in0=fix, in1=ub[:, g, 16:32], op=ALU.subtract),
                           reads=[B_fix, B_ub], writes=[B_dp[g]])
                    b = nb()
                    mmg(PE, PS[:, b, 0:T], [(Wpool[:, g, :], dp[:, g, :])], reads=[B_RA, B_dp[g]], writes=[B_ps[b]])
                    op(DVE, lambda e, b=b, g=g: e.scalar_tensor_tensor(
                        out=ys[:, g, :], in0=PS[:, b, 0:T], scalar=cvec[:, cv + 24 + g:cv + 25 + g], in1=onesf[:, 0:T],
                        op0=ALU.mult, op1=ALU.mult),
                       reads=[B_ps[b], B_const], writes=[B_ys[g]])
            if MIXSTOP >= 6:
                er = [0]
                for blk in range(2):
                    n = 2 * i + blk
                    first = (n == 0)
                    lo = 128 if first else 0
                    qcols = slice(blk * 128, (blk + 1) * 128)
                    pcols = slice(blk * 128, blk * 128 + 128)
                    ccols = slice((blk + 1) * 128, (blk + 2) * 128)
                    for hq in range(8):
                        kv, cq, half = hq // 4, hq // 2, hq % 2
                        pr = slice(64 * half, 64 * half + 64)
                        b = nb()
                        if not first:
                            mmg(PE, PS[:, b, 0:128], [(kT[pr, kv, pcols], qT[pr, cq, qcols])],
                                reads=[B_kT, B_kTh, B_qT], writes=[B_ps[b]])
                        mmg(PE, PS[:, b, 128:256], [(kT[pr, kv, ccols], qT[pr, cq, qcols])],
                            reads=[B_kT, B_kTh, B_qT], writes=[B_ps[b]])
                        r = er[0] % 3
                        er[0] += 1
                        op(ACT, lambda e, b=b, r=r, lo=lo: e.activation(out=ebuf[r][:, lo:256], in_=PS[:, b, lo:256],
                                                                        func=AF.Exp, scale=0.125),
                           reads=[B_ps[b]], writes=[B_e[r]])
                        op(POOL, lambda e, r=r, lo=lo: e.tensor_tensor(out=pT[r][:, lo:256], in0=ebuf[r][:, lo:256],
                                                                       in1=mask[:, lo:256], op=ALU.mult),
                           reads=[B_e[r], B_const], writes=[B_pT[r]])
                        ob = 1 + hq // 4
                        oap = PS[:, ob, 0:260].rearrange("p (a d) -> p a d", a=4)[:, hq % 4, :]
                        pairs = []
                        if not first:
                            pairs.append((pT[r][:, 0:128], vb[:, blk, kv, 0:65]))
                        pairs.append((pT[r][:, 128:256], vb[:, blk + 1, kv, 0:65]))
                        mmg(PE, oap, pairs, reads=[B_pT[r], B_vb, B_vbh], writes=[B_ps[ob]])
                    for ob in (1, 2):
                        o4 = PS[:, ob, 0:260].rearrange("p (a d) -> p a d", a=4)
                        h0 = 4 * (ob - 1)
                        op(DVE, lambda e, o4=o4, h0=h0: e.tensor_tensor(
                            out=den[:, h0:h0 + 4], in0=o4[:, :, 64], in1=esink[:, 8 * l + h0:8 * l + h0 + 4], op=ALU.add),
                           reads=[B_ps[ob], B_esink], writes=[B_den])
                    op(DVE, lambda e: e.reciprocal(out=rec, in_=den), reads=[B_den], writes=[B_rec])
                    for hq in range(8):
                        ob = 1 + hq // 4
                        o4 = PS[:, ob, 0:260].rearrange("p (a d) -> p a d", a=4)
                        op(DVE, lambda e, o4=o4, hq=hq, blk=blk: e.scalar_tensor_tensor(
                            out=on[blk][:, hq, :], in0=o4[:, hq % 4, 0:64], scalar=rec[:, hq:hq + 1], in1=onesf[:, 0:64],
                            op0=ALU.mult, op1=ALU.mult),
                           reads=[B_ps[ob], B_rec, B_const], writes=[B_on[blk]])
                    pst = PS[:, 3, 0:256].bitcast(BF16).rearrange("p (a d) -> p a d", a=4)
                    _pre(PE, _flat([B_on[blk], B_const]), [B_ps[3]])
                    ins = None
                    for c in range(4):
                        ins = PE.e.transpose(out=pst[:, c, :], in_=on[blk][:, 2 * c:2 * c + 2, :].rearrange("p a d -> p (a d)"),
                                             identity=ident)
                    PE.cnt += 1
                    ins.then_inc(PE.sem, 1)
                    _post((PE.sem, PE.cnt), _flat([B_on[blk], B_const]), [B_ps[3]])
                    op(ACT, lambda e, blk=blk: e.activation(out=oT[:, :, blk * 128:(blk + 1) * 128], in_=pst, func=AF.Copy),
                       reads=[B_ps[3]], writes=[B_oT[blk]])
                    if blk == 0 and i + 1 < NT:
                        norm_h(i + 1, gcol)
            if MIXSTOP >= 7:
                for c in range(8):
                    b1 = nb()
                    b2 = nb()
                    cc = slice(c * 128, (c + 1) * 128)
                    mmg(PE, PS[:, b1, 0:T], [(Wpb[:, g, cc], ys[:, g, :]) for g in range(4)],
                        reads=[B_RA, B_ys], writes=[B_ps[b1]])
                    mmg(PE, PS[:, b1, T:2 * T], [(Wab[:, g, cc], oT[:, g, :]) for g in range(4)],
                        reads=[B_RA, B_oT], writes=[B_ps[b1]])
                    mmg(PE, PS[:, b2, 0:T], [(Win[:, k, C_GP + c * 128:C_GP + (c + 1) * 128], h[:, k, :]) for k in range(8)],
                        reads=[B_RA, Bh], writes=[B_ps[b2]])
                    mmg(PE, PS[:, b2, T:2 * T], [(Win[:, k, C_GA + c * 128:C_GA + (c + 1) * 128], h[:, k, :]) for k in range(8)],
                        reads=[B_RA, Bh], writes=[B_ps[b2]])
                    r = c % 2
                    op(ACT, lambda e, b2=b2, r=r: e.activation(out=tg[r], in_=PS[:, b2, :], func=AF.Tanh, scale=0.5),
                       reads=[B_ps[b2]], writes=[B_tg[r]])
                    op(DVE, lambda e, b1=b1, r=r: e.scalar_tensor_tensor(
                        out=m1[r], in0=tg[r][:, 0:T], scalar=1.0, in1=PS[:, b1, 0:T], op0=ALU.add, op1=ALU.mult),
                       reads=[B_tg[r], B_ps[b1]], writes=[B_m1[r]])
                    op(DVE, lambda e, b1=b1, r=r: e.scalar_tensor_tensor(
                        out=m2[r], in0=tg[r][:, T:2 * T], scalar=1.0, in1=PS[:, b1, T:2 * T], op0=ALU.add, op1=ALU.mult),
                       reads=[B_tg[r], B_ps[b1]], writes=[B_m2[r]])
                    op(POOL, lambda e, c=c, r=r: e.tensor_tensor(out=mb[:, c, :], in0=m1[r], in1=m2[r], op=ALU.add),
                       reads=[B_m1[r], B_m2[r]], writes=[B_m[c]])
                for c2 in range(8):
                    b = nb()
                    mmg(PE, PS[:, b, 0:T], [(Wout[:, c, c2 * 128:(c2 + 1) * 128], mb[:, c, :]) for c in range(8)],
                        reads=[B_RA, B_m], writes=[B_ps[b]])
                    resid(par, c2, b, B_ps[b])
            store(i, dst_v, dst_bufs)
            if i + 2 < NT:
                load(i + 2, src_v, src_bufs)
                load_cs(i + 2)
                norm_sq(i + 2)
        for bb in _flat(all_rb):
            for ev in list(bb.r.values()) + ([bb.w] if bb.w else []):
                k = ev[0].num
                if k not in B_RB.r or B_RB.r[k][1] < ev[1]:
                    B_RB.r[k] = ev

    if phases is None:
        phases = []
        for l in range(L):
            phases += [("ffn1", l), ("mix", l), ("ffn2", l)]
    np_ = len(phases)
    engs = [PE, ACT, DVE, POOL, SP]
    for pi, (kind, l) in enumerate(phases):
        if pi > 0:
            for E_ in engs:
                for F_ in engs:
                    if F_ is not E_ and F_.cnt:
                        E_.wait((F_.sem, F_.cnt))
        src_v, src_bufs = (xview(xT), B_in) if pi == 0 else (xview(xs), B_xs)
        dst_v, dst_bufs = (xview(outT), B_out) if pi == np_ - 1 else (xview(xs), B_xs)
        if kind == "ffn1":
            ffn_phase(src_v, src_bufs, dst_v, dst_bufs, w1gu[l], w1d[l], NVL * l + 0)
        elif kind == "ffn2":
            ffn_phase(src_v, src_bufs, dst_v, dst_bufs, w2gu[l], w2d[l], NVL * l + 16)
        else:
            mix_phase(src_v, src_bufs, dst_v, dst_bufs, l)
    for d_ in ds_xs:
        if d_.cnt:
            SP.e.wait_ge(d_.sem, d_.cnt)
    return nc


def host_consts(S):
    c = np.zeros((128, 768), np.float32)
    c[:, 0:128] = 1.0 / 1024.0
    for p in range(128):
        b = (p // 64) * 64
        c[p, 128 + b:128 + b + 64] = 1.0 / 64.0
    for d in range(128):
        dd = d % 64
        if dd < 8:
            c[d + 8, 256 + d] = 1.0
        elif dd < 16:
            c[d - 8, 256 + d] = 1.0
    c[:, 384:512] = np.eye(128, dtype=np.float32)
    k = np.arange(128)[:, None]
    j = np.arange(128)[None, :]
    c[:, 512:640] = (k > j).astype(np.float32)
    c[:, 640:768] = (k <= j).astype(np.float32)
    pos = np.arange(S, dtype=np.float32)
    inv_freq = (np.float32(500000.0) ** (-np.arange(0, 16, 2, dtype=np.float32) / np.float32(16))).astype(np.float32)
    ang = (pos[:, None] * inv_freq[None, :]).astype(np.float32)
    cos, sin = np.cos(ang).astype(np.float32), np.sin(ang).astype(np.float32)
    rope = np.zeros((128, 2, S), np.float32)
    rope[:, 0, :] = 1.0
    for p in range(128):
        dd = p % 64
        if dd < 8:
            rope[p, 0] = cos[:, dd]
            rope[p, 1] = -sin[:, dd]
        elif dd < 16:
            rope[p, 0] = cos[:, dd - 8]
            rope[p, 1] = sin[:, dd - 8]
    invc = np.zeros((128, 4, 16), np.float32)
    for g, w in enumerate(POOL_W):
        invc[:, g, :] = 1.0 / np.minimum(np.arange(16) + 1, w).astype(np.float32)
    return c, rope, invc.reshape(128, 64)


def host_cvec(L, ln_ffn1, ln_mix, ln_ffn2, pool_scale, q_norm, k_norm, sinks):
    cv = np.zeros((128, NVL * L), np.float32)
    for l in range(L):
        o = NVL * l
        cv[:, o + 0:o + 8] = ln_ffn1[l].reshape(8, 128).T
        cv[:, o + 8:o + 16] = ln_mix[l].reshape(8, 128).T
        cv[:, o + 16:o + 24] = ln_ffn2[l].reshape(8, 128).T
        cv[:, o + 24:o + 28] = pool_scale[l].reshape(4, 128).T
        cv[:, o + 28] = np.tile(q_norm[l], 2)
        cv[:, o + 29] = np.tile(k_norm[l], 2)
        cv[:, o + 30:o + 38] = sinks[l][None, :]
    return cv


def host_win(w_in):
    L = w_in.shape[0]
    r = np.empty((L, D, WIN_COLS), np.float32)
    r[:, :, 0:1024] = w_in[:, :, 0:1024]
    k0 = w_in[:, :, 1024:1088]
    k1 = w_in[:, :, 1088:1152]
    r[:, :, 1024:1088] = k0
    r[:, :, 1088:1152] = k0
    r[:, :, 1152:1216] = k1
    r[:, :, 1216:1280] = k1
    r[:, :, 1280:1408] = w_in[:, :, 1152:1280]
    r[:, :, 1408:3456] = w_in[:, :, 1280:3328]
    return r


def make_in_maps(x, ln_ffn1, w_ffn1_gu, w_ffn1_down, ln_mix, w_in, pool_w, pool_scale,
                 w_pool_branch, q_norm, k_norm, sinks, w_attn_branch, w_out,
                 ln_ffn2, w_ffn2_gu, w_ffn2_down):
    f = lambda a: np.ascontiguousarray(np.asarray(a, dtype=np.float32))
    B, S, _ = x.shape
    L = w_in.shape[0]
    consts, rope, invc = host_consts(S)
    shared = {
        "w1gu": f(w_ffn1_gu), "w1d": f(w_ffn1_down), "w2gu": f(w_ffn2_gu), "w2d": f(w_ffn2_down),
        "win": host_win(f(w_in)), "wpool": f(pool_w), "wpb": f(w_pool_branch), "wab": f(w_attn_branch),
        "wout": f(w_out),
        "cvec": host_cvec(L, f(ln_ffn1), f(ln_mix), f(ln_ffn2), f(pool_scale), f(q_norm), f(k_norm), f(sinks)),
        "rope": rope, "consts": consts, "invc": invc,
    }
    maps = []
    for b in range(B):
        m = dict(shared)
        m["xT"] = np.ascontiguousarray(np.asarray(x[b], dtype=np.float32).T)
        maps.append(m)
    return maps, S, L


def kernel(**inputs):
    maps, S, L = make_in_maps(**inputs)
    nc = build(S, L)
    res = run_bass_kernel_spmd(nc, maps, core_ids=list(range(len(maps))))
    out = np.stack([np.ascontiguousarray(r["outT"].T) for r in res.results], axis=0)
    return out.astype(np.float32)
```
